# Optimizing a Trainium2 kernel written in Bass

```python
import jax, jax.numpy as jnp
from jax import lax
import numpy as np

D_MODEL = 4096
BATCH = 1
SEQ = 8192
DEPTH = 1

N_MEM = 256
D_MIX = D_MODEL
HEAD_DIM = 128
MLA_HEADS = D_MIX // 2 // HEAD_DIM
MLA_NOPE = 128
MLA_ROPE = 64
MLA_V = HEAD_DIM
MLA_Q_RANK = D_MODEL // 4
MLA_KV_RANK = 512
GDN_HEADS = (D_MIX - MLA_HEADS * MLA_V) // HEAD_DIM
GDN_DK = HEAD_DIM
GDN_DV = HEAD_DIM
GDN_CONV = 4
GDN_CHUNK = 64
Q_BLOCK = 128
XATTN_HEADS = 4
XATTN_DIM = D_MODEL // XATTN_HEADS
D_FF = 4 * D_MODEL
ROPE_BASE = 10000.0
NORM_EPS = 1e-6
L2_EPS = 1e-6
GDN_QKV = GDN_HEADS * (2 * GDN_DK + GDN_DV)
IN_SIZES = (MLA_Q_RANK, MLA_KV_RANK, MLA_ROPE, GDN_QKV, GDN_HEADS, GDN_HEADS, GDN_HEADS * GDN_DV)
D_IN = int(sum(IN_SIZES))
IN_SPLITS = tuple(int(v) for v in np.cumsum(IN_SIZES)[:-1])

kernel_name = 'hybrid_mla_gdn_parallel_heads_block'


def _rms_norm(x, w):
    xf = x.astype(jnp.float32)
    y = xf * lax.rsqrt(jnp.mean(xf * xf, axis=-1, keepdims=True) + NORM_EPS) * w.astype(jnp.float32)
    return y.astype(x.dtype)


def _l2_normalize(x):
    xf = x.astype(jnp.float32)
    return xf * lax.rsqrt(jnp.sum(xf * xf, axis=-1, keepdims=True) + L2_EPS)


def _rope(x, positions):
    half = x.shape[-1] // 2
    inv_freq = ROPE_BASE ** (-jnp.arange(half, dtype=jnp.float32) / half)
    ang = positions.astype(jnp.float32)[..., None] * inv_freq
    cos = jnp.cos(ang)[:, :, None, :]
    sin = jnp.sin(ang)[:, :, None, :]
    x1 = x[..., :half].astype(jnp.float32)
    x2 = x[..., half:].astype(jnp.float32)
    out = jnp.concatenate([x1 * cos - x2 * sin, x2 * cos + x1 * sin], axis=-1)
    return out.astype(x.dtype)


def _causal_block_attention(q, k, v, scale):
    B, S, H, Dq = q.shape
    Dv = v.shape[-1]
    nb = S // Q_BLOCK
    q_blocks = jnp.moveaxis(q.reshape(B, nb, Q_BLOCK, H, Dq), 1, 0)
    starts = jnp.arange(nb, dtype=jnp.int32) * Q_BLOCK
    k_pos = jnp.arange(S, dtype=jnp.int32)

    def one_block(args):
        q_blk, start = args
        s = jnp.einsum('bqhd,bkhd->bhqk', q_blk, k, preferred_element_type=jnp.float32) * scale
        q_pos = start + jnp.arange(Q_BLOCK, dtype=jnp.int32)
        causal = k_pos[None, :] <= q_pos[:, None]
        s = jnp.where(causal[None, None], s, -jnp.inf)
        p = jax.nn.softmax(s, axis=-1).astype(v.dtype)
        return jnp.einsum('bhqk,bkhd->bqhd', p, v)

    out = lax.map(one_block, (q_blocks, starts))
    return jnp.moveaxis(out, 0, 1).reshape(B, S, H, Dv)


def _causal_depthwise_conv(x, w):
    K, C = w.shape
    return lax.conv_general_dilated(x, w[:, None, :].astype(x.dtype), window_strides=(1,),
                                    padding=[(K - 1, 0)], dimension_numbers=('NWC', 'WIO', 'NWC'),
                                    feature_group_count=C)


def _gated_delta_rule_chunked(q, k, v, g, beta):
    B, S, H, Dk = k.shape
    Dv = v.shape[-1]
    N, C = S // GDN_CHUNK, GDN_CHUNK
    f32 = jnp.float32

    def chunks(t):
        t = jnp.swapaxes(t.astype(f32), 1, 2)
        return t.reshape((B, H, N, C) + t.shape[3:])

    qc = chunks(q) * (Dk ** -0.5)
    kc, vc, gc, bc = chunks(k), chunks(v), chunks(g), chunks(beta)
    gc = jnp.cumsum(gc, axis=-1)
    tril = jnp.tril(jnp.ones((C, C), dtype=bool))
    strict = jnp.tril(jnp.ones((C, C), dtype=bool), k=-1)
    decay = jnp.exp(jnp.where(tril, gc[..., :, None] - gc[..., None, :], -jnp.inf))
    k_beta = kc * bc[..., None]
    v_beta = vc * bc[..., None]
    a_mat = jnp.where(strict, jnp.einsum('bhncd,bhnjd->bhncj', k_beta, kc) * decay, 0.0)
    lower = a_mat + jnp.eye(C, dtype=f32)
    u = lax.linalg.triangular_solve(lower, v_beta, left_side=True, lower=True, unit_diagonal=True)
    w = lax.linalg.triangular_solve(lower, k_beta * jnp.exp(gc)[..., None], left_side=True, lower=True,
                                    unit_diagonal=True)
    qk = jnp.einsum('bhncd,bhnjd->bhncj', qc, kc) * decay
    g_last = gc[..., -1]
    q_dec = qc * jnp.exp(gc)[..., None]
    k_tail = kc * jnp.exp(g_last[..., None] - gc)[..., None]
    chunk_decay = jnp.exp(g_last)

    def step(state, xs):
        u_n, w_n, qk_n, qd_n, kt_n, cd_n = xs
        v_new = u_n - jnp.einsum('bhcd,bhde->bhce', w_n, state)
        o_n = jnp.einsum('bhcd,bhde->bhce', qd_n, state) + jnp.einsum('bhcj,bhje->bhce', qk_n, v_new)
        state = state * cd_n[..., None, None] + jnp.einsum('bhcd,bhce->bhde', kt_n, v_new)
        return state, o_n

    xs = tuple(jnp.moveaxis(t, 2, 0) for t in (u, w, qk, q_dec, k_tail, chunk_decay))
    state0 = jnp.zeros((B, H, Dk, Dv), f32)
    _, o = lax.scan(step, state0, xs)
    o = jnp.transpose(o, (1, 0, 3, 2, 4)).reshape(B, S, H, Dv)
    return o.astype(v.dtype)


def _hybrid_mixer(xn, positions, w_in, q_norm_w, w_q_b, kv_norm_w, w_kv_b,
                  conv_w, a_log, dt_bias, gdn_norm_w, w_out):
    B, S, _ = xn.shape
    proj = xn @ w_in
    c_q, c_kv, k_pe, qkv, a_logit, b_logit, gate = jnp.split(proj, IN_SPLITS, axis=-1)

    q = (_rms_norm(c_q, q_norm_w) @ w_q_b).reshape(B, S, MLA_HEADS, MLA_NOPE + MLA_ROPE)
    kv = (_rms_norm(c_kv, kv_norm_w) @ w_kv_b).reshape(B, S, MLA_HEADS, MLA_NOPE + MLA_V)
    q_nope, q_pe = q[..., :MLA_NOPE], _rope(q[..., MLA_NOPE:], positions)
    k_nope, v = kv[..., :MLA_NOPE], kv[..., MLA_NOPE:]
    k_pe = _rope(k_pe[:, :, None, :], positions)
    q_full = jnp.concatenate([q_nope, q_pe], axis=-1)
    k_full = jnp.concatenate([k_nope, jnp.broadcast_to(k_pe, (B, S, MLA_HEADS, MLA_ROPE))], axis=-1)
    mla_out = _causal_block_attention(q_full, k_full, v, (MLA_NOPE + MLA_ROPE) ** -0.5)

    qkv = jax.nn.silu(_causal_depthwise_conv(qkv, conv_w))
    gq, gk, gv = jnp.split(qkv, (GDN_HEADS * GDN_DK, 2 * GDN_HEADS * GDN_DK), axis=-1)
    gq = _l2_normalize(gq.reshape(B, S, GDN_HEADS, GDN_DK))
    gk = _l2_normalize(gk.reshape(B, S, GDN_HEADS, GDN_DK))
    gv = gv.reshape(B, S, GDN_HEADS, GDN_DV)
    log_decay = -jnp.exp(a_log.astype(jnp.float32)) * jax.nn.softplus(
        a_logit.astype(jnp.float32) + dt_bias.astype(jnp.float32))
    beta = jax.nn.sigmoid(b_logit.astype(jnp.float32))
    gdn_out = _gated_delta_rule_chunked(gq, gk, gv, log_decay, beta)
    gdn_out = _rms_norm(gdn_out, gdn_norm_w) * jax.nn.silu(gate.reshape(B, S, GDN_HEADS, GDN_DV))

    mix = jnp.concatenate([mla_out.reshape(B, S, MLA_HEADS * MLA_V),
                           gdn_out.reshape(B, S, GDN_HEADS * GDN_DV)], axis=-1)
    return mix @ w_out


def _memory_cross_attention(hn, memn, wq, wk, wv, wo):
    B, S, _ = hn.shape
    M = memn.shape[1]
    q = (hn @ wq).reshape(B, S, XATTN_HEADS, XATTN_DIM)
    k = (memn @ wk).reshape(B, M, XATTN_HEADS, XATTN_DIM)
    v = (memn @ wv).reshape(B, M, XATTN_HEADS, XATTN_DIM)
    s = jnp.einsum('bqhd,bkhd->bhqk', q, k, preferred_element_type=jnp.float32) * (XATTN_DIM ** -0.5)
    p = jax.nn.softmax(s, axis=-1).astype(v.dtype)
    o = jnp.einsum('bhqk,bkhd->bqhd', p, v).reshape(B, S, XATTN_HEADS * XATTN_DIM)
    return o @ wo


def _squared_relu_mlp(hn, w_up, w_down):
    return jnp.square(jax.nn.relu(hn @ w_up)) @ w_down


def setup_inputs(seed: int = 0) -> dict:
    key = jax.random.key(seed)
    ks = jax.random.split(key, 26)
    f32 = jnp.float32

    def lin(k, fan_in, fan_out):
        return jax.random.normal(k, (DEPTH, fan_in, fan_out), f32) * fan_in ** -0.5

    def gain(k, n):
        return 1.0 + 0.02 * jax.random.normal(k, (DEPTH, n), f32)

    x = jax.random.normal(ks[0], (BATCH, SEQ, D_MODEL), f32)
    mem = jax.random.normal(ks[1], (BATCH, N_MEM, D_MODEL), f32)
    start = jax.random.randint(ks[2], (BATCH, 1), 0, 4096, dtype=jnp.int32)
    positions = start + jnp.arange(SEQ, dtype=jnp.int32)[None, :]
    dt = jnp.exp(jax.random.uniform(ks[11], (DEPTH, GDN_HEADS), f32, minval=np.log(1e-3), maxval=np.log(1e-1)))
    return {
        'x': x,
        'mem': mem,
        'positions': positions,
        'attn_norm_w': gain(ks[3], D_MODEL),
        'w_in': lin(ks[4], D_MODEL, D_IN),
        'mla_q_norm_w': gain(ks[5], MLA_Q_RANK),
        'mla_w_q_b': lin(ks[6], MLA_Q_RANK, MLA_HEADS * (MLA_NOPE + MLA_ROPE)),
        'mla_kv_norm_w': gain(ks[7], MLA_KV_RANK),
        'mla_w_kv_b': lin(ks[8], MLA_KV_RANK, MLA_HEADS * (MLA_NOPE + MLA_V)),
        'gdn_conv_w': jax.random.normal(ks[9], (DEPTH, GDN_CONV, GDN_QKV), f32) * GDN_CONV ** -0.5,
        'gdn_a_log': jnp.log(jax.random.uniform(ks[10], (DEPTH, GDN_HEADS), f32, minval=1.0, maxval=16.0)),
        'gdn_dt_bias': dt + jnp.log(-jnp.expm1(-dt)),
        'gdn_norm_w': gain(ks[12], GDN_DV),
        'w_out': lin(ks[13], D_MIX, D_MODEL),
        'xattn_norm_w': gain(ks[14], D_MODEL),
        'mem_norm_w': gain(ks[15], D_MODEL),
        'xattn_wq': lin(ks[16], D_MODEL, XATTN_HEADS * XATTN_DIM),
        'xattn_wk': lin(ks[17], D_MODEL, XATTN_HEADS * XATTN_DIM),
        'xattn_wv': lin(ks[18], D_MODEL, XATTN_HEADS * XATTN_DIM),
        'xattn_wo': lin(ks[19], XATTN_HEADS * XATTN_DIM, D_MODEL),
        'mlp_norm_w': gain(ks[20], D_MODEL),
        'mlp_w_up': lin(ks[21], D_MODEL, D_FF),
        'mlp_w_down': lin(ks[22], D_FF, D_MODEL),
        'final_norm_w': 1.0 + 0.02 * jax.random.normal(ks[23], (D_MODEL,), f32),
    }


def reference(x, mem, positions, attn_norm_w, w_in, mla_q_norm_w, mla_w_q_b, mla_kv_norm_w, mla_w_kv_b,
              gdn_conv_w, gdn_a_log, gdn_dt_bias, gdn_norm_w, w_out, xattn_norm_w, mem_norm_w,
              xattn_wq, xattn_wk, xattn_wv, xattn_wo, mlp_norm_w, mlp_w_up, mlp_w_down, final_norm_w):
    h = x
    for l in range(DEPTH):
        h = h + _hybrid_mixer(_rms_norm(h, attn_norm_w[l]), positions, w_in[l],
                              mla_q_norm_w[l], mla_w_q_b[l], mla_kv_norm_w[l], mla_w_kv_b[l],
                              gdn_conv_w[l], gdn_a_log[l], gdn_dt_bias[l], gdn_norm_w[l], w_out[l])
        h = h + _memory_cross_attention(_rms_norm(h, xattn_norm_w[l]), _rms_norm(mem, mem_norm_w[l]),
                                        xattn_wq[l], xattn_wk[l], xattn_wv[l], xattn_wo[l])
        h = h + _squared_relu_mlp(_rms_norm(h, mlp_norm_w[l]), mlp_w_up[l], mlp_w_down[l])
    return _rms_norm(h, final_norm_w)
```

```python
import contextlib
import numpy as np
import ml_dtypes
import concourse.bass as bass
import concourse.mybir as mybir
from concourse.bass_utils import run_bass_kernel_spmd

F32 = mybir.dt.float32
BF16 = mybir.dt.bfloat16
I32 = mybir.dt.int32
ALU = mybir.AluOpType
AF = mybir.ActivationFunctionType
AX = mybir.AxisListType

NC = 8
SEQ = 8192
T = SEQ // NC
D = 4096
TT = T // 128
TCH = T // 512
EPS = 1e-6
DEBUG = {}


class _Op:
    __slots__ = ("eng", "fn", "deps", "dma", "sig", "semval", "dsem", "dval", "cc")

    def __init__(self, eng, fn, dma, cc=False):
        self.eng = eng
        self.fn = fn
        self.deps = []
        self.dma = dma
        self.cc = cc
        self.sig = False
        self.semval = None
        self.dsem = None
        self.dval = None


class Sched:
    ENG = ("pe", "dve", "act", "pool", "sp")

    def __init__(self, nc, stack, ndma=6):
        self.nc = nc
        self.e = {"pe": nc.tensor, "dve": nc.vector, "act": nc.scalar,
                  "pool": nc.gpsimd, "sp": nc.sync}
        self.sem = {k: stack.enter_context(nc.semaphore("s_" + k)) for k in self.ENG}
        self.K = ndma
        self.dsem = {k: [stack.enter_context(nc.semaphore("d_%s%d" % (k, i)))
                         for i in range(ndma)] for k in ("sp", "act", "pool")}
        self.stack = stack
        self.cnt = {k: 0 for k in self.ENG}
        self.dcnt = {k: 0 for k in self.dsem}
        self.ccs = []
        self.ccw = {}
        self.waited = {}
        self.pending = []
        self.lastw = {}
        self.readers = {}
        self.nops = 0

    def op(self, eng, fn, r=(), w=(), dma=False, cc=False):
        o = _Op(eng, fn, dma, cc)
        deps = {}
        for k in r:
            y = self.lastw.get(k) or self.ccw.get(k)
            if y is not None:
                deps[id(y)] = y
        for k in w:
            y = self.lastw.get(k) or self.ccw.get(k)
            if y is not None:
                deps[id(y)] = y
            for y in self.readers.get(k, ()):
                deps[id(y)] = y
            if cc:
                self.ccw[k] = o
        o.deps = list(deps.values())
        for k in r:
            self.readers.setdefault(k, []).append(o)
        for k in w:
            self.lastw[k] = o
            self.readers[k] = []
        self.pending.append(o)
        return o

    def i(self, eng, meth, rw, *args, **kw):
        return self.op(eng, lambda e: getattr(e, meth)(*args, **kw), rw.get("r", ()), rw.get("w", ()))

    def dma(self, q, out, in_, r=(), w=(), **kw):
        def f(e):
            try:
                return e.dma_start(out=out, in_=in_, **kw)
            except Exception:
                print("DMA FAIL", q, "out", out.shape, out.ap, "in", in_.shape, in_.ap, r, w)
                raise
        return self.op(q, f, r, w, dma=True)

    def _wait(self, eng, sem, val):
        key = (eng, id(sem))
        if self.waited.get(key, 0) < val:
            self.e[eng].wait_ge(sem, val)
            self.waited[key] = val

    def flush(self, barrier=True):
        ops = self.pending
        self.pending = []
        last = {}
        for o in ops:
            for y in o.deps:
                if y.dma:
                    continue
                if y.eng == "pe" and o.eng == "pe" and not o.dma:
                    continue
                y.sig = True
            if not o.dma:
                last[o.eng] = o
        if barrier:
            for o in last.values():
                o.sig = True
        for o in ops:
            if o.dma:
                if o.cc:
                    o.dsem = self.stack.enter_context(self.nc.semaphore("cc%d" % self.nops))
                    o.dval = 1
                    self.ccs.append(o)
                else:
                    i = self.dcnt[o.eng]
                    self.dcnt[o.eng] += 1
                    o.dsem = self.dsem[o.eng][i % self.K]
                    o.dval = 16 * (i // self.K + 1)
            elif o.sig:
                self.cnt[o.eng] += 1
                o.semval = self.cnt[o.eng]
            self.nops += 1
        for o in ops:
            a = o.eng
            for y in o.deps:
                if y.dma:
                    self._wait(a, y.dsem, y.dval)
                elif y.semval is not None:
                    if y.eng == "pe" and a == "pe" and not o.dma:
                        continue
                    self._wait(a, self.sem[y.eng], y.semval)
            if o.dma and not o.cc and o.dval > 16:
                self._wait(a, o.dsem, o.dval - 16)
            ins = o.fn(self.e[a])
            if o.dma:
                if o.cc:
                    ins.then_inc(o.dsem)
                else:
                    ins.then_inc(o.dsem, 16)
            elif o.sig:
                ins.then_inc(self.sem[a], 1)
        if barrier:
            self.barrier()

    def barrier(self):
        for a in self.ENG:
            for b in self.ENG:
                if b != a and self.cnt[b] > 0:
                    self._wait(a, self.sem[b], self.cnt[b])
            for q, lst in self.dsem.items():
                n = self.dcnt[q]
                for j, s in enumerate(lst):
                    uses = (n - j + self.K - 1) // self.K if n > j else 0
                    if uses > 0:
                        self._wait(a, s, 16 * uses)
        self.lastw = {}
        self.readers = {}


C_ID, C_ONE, C_MISC, C_U, C_MT, C_MS, C_CM = 0, 128, 256, 264, 328, 392, 456
C_N = 456 + 4 * 512


def make_consts():
    c = np.zeros((128, C_N), np.float32)
    c[:, C_ID:C_ID + 128] = np.eye(128, dtype=np.float32)
    c[:, C_ONE:C_ONE + 128] = 1.0
    p = np.arange(128)
    invf = 10000.0 ** (-(p % 32).astype(np.float64) / 32.0)
    c[:, C_MISC + 0] = (invf / (2 * np.pi)).astype(np.float32)
    c[:, C_MISC + 1] = np.where((p % 64) < 32, -2 * np.pi, 2 * np.pi).astype(np.float32)
    c[:, C_MISC + 2] = np.float32(2 * np.pi)
    c[:, C_MISC + 3] = EPS
    i = np.arange(64)
    c[:64, C_U:C_U + 64] = (i[:, None] <= i[None, :]).astype(np.float32)
    c[:64, C_MT:C_MT + 64] = np.where(i[None, :] >= i[:, None], 0.0, -30000.0)
    c[:64, C_MS:C_MS + 64] = np.where(i[:, None] > i[None, :], 0.0, -30000.0)
    k = np.arange(128)[:, None]
    q = np.arange(512)[None, :]
    for j in range(4):
        c[:, C_CM + 512 * j:C_CM + 512 * (j + 1)] = ((128 * j + k) <= q).astype(np.float32)
    return c


def build(stage=99):
    nc = bass.Bass("TRN2", target_bir_lowering=False)

    def din(name, shape, dt=F32):
        return nc.dram_tensor(name, list(shape), dt, kind="ExternalInput").ap()

    def dtmp(name, shape, dt=F32):
        return nc.dram_tensor(name, list(shape), dt)

    x_d = din("x", [T, D])
    pos_d = din("pos", [1, T], I32)
    cst_d = din("cst", [128, C_N])
    nw_attn = din("nw_attn", [D])
    w_cq = din("w_cq", [D, 1536])
    w_kpe = din("w_kpe", [D, 128])
    w_qkv = din("w_qkv", [D, 6144])
    w_gate = din("w_gate", [D, 2048])
    w_ab = din("w_ab", [D, 32])
    nw_q = din("nw_q", [1024])
    nw_kv = din("nw_kv", [512])
    w_qn = din("w_qn", [1024, 2048])
    w_qpe = din("w_qpe", [1024, 1024])
    w_qrot = din("w_qrot", [1024, 1024])
    w_kn = din("w_kn", [512, 2048])
    w_v = din("w_v", [512, 2048])
    a_log = din("a_log", [1, 16])
    dt_bias = din("dt_bias", [1, 16])

    agA = dtmp("agA", [5184, T], BF16)
    agA_o = dtmp("agA_o", [NC * 5184, T], BF16)
    agV = dtmp("agV", [T, 2048], BF16)
    agV_o = dtmp("agV_o", [NC * T, 2048], BF16)
    agG = dtmp("agG", [6144, T], BF16)
    agG_o = dtmp("agG_o", [NC * 6144, T], BF16)
    agAB = dtmp("agAB", [T, 32], F32)
    agAB_o = dtmp("agAB_o", [NC * T, 32], F32)
    gate_d = dtmp("gate_d", [T, 2048], F32)

    dbg = {}
    if "A" in DEBUG:
        dbg["agA"] = nc.dram_tensor("dbg_agA", [5184, T], BF16, kind="ExternalOutput").ap()
        dbg["agV"] = nc.dram_tensor("dbg_agV", [T, 2048], BF16, kind="ExternalOutput").ap()
        dbg["agG"] = nc.dram_tensor("dbg_agG", [6144, T], BF16, kind="ExternalOutput").ap()
        dbg["agAB"] = nc.dram_tensor("dbg_agAB", [T, 32], F32, kind="ExternalOutput").ap()
        dbg["gate"] = nc.dram_tensor("dbg_gate", [T, 2048], F32, kind="ExternalOutput").ap()

    with contextlib.ExitStack() as g:
        S = Sched(nc, g)
        sb = lambda st, name, shape, dt=F32: st.enter_context(nc.sbuf_tensor("sb_" + name, list(shape), dt))
        ps = [g.enter_context(nc.psum_tensor("ps%d" % i, [128, 512], F32)) for i in range(7)]
        pst = g.enter_context(nc.psum_tensor("pst", [128, 1024], BF16))
        bank = [0]

        def nb():
            bank[0] = (bank[0] + 1) % 7
            return bank[0]

        cst = sb(g, "cst", [128, C_N])
        idb = sb(g, "idb", [128, 128], BF16)
        oneb = sb(g, "oneb", [128, 128], BF16)
        S.dma("sp", cst[:, :], cst_d, w=["cst"])
        S.i("dve", "tensor_copy", dict(r=["cst"], w=["idb"]), out=idb[:, :], in_=cst[:, C_ID:C_ID + 128])
        S.i("dve", "tensor_copy", dict(r=["cst"], w=["oneb"]), out=oneb[:, :], in_=cst[:, C_ONE:C_ONE + 128])
        epsc = cst[:, C_MISC + 3:C_MISC + 4]

        def rstd_from_ss(eng_pref, out, ss, n, rd, wr):
            S.i("act", "activation", dict(r=["cst"] + rd, w=wr), out=out, in_=ss, func=AF.Sqrt, bias=epsc[0:out.shape[0], :], scale=1.0 / n)
            S.i("dve", "reciprocal", dict(r=[], w=wr), out=out, in_=out)

        def load_w(buf, name, Wd, K, c0, width):
            KC = K // 128
            S.dma("pool", buf[:, 0:KC, 0:width], Wd.rearrange("(c p) n -> p c n", p=128)[:, :, c0:c0 + width],
                  w=[name])

        def gemm_tools(wb, BW):
            nwb = len(wb)
            wi = [0]
            si = [0]

            def blocks(Wd, K, ncols, bw=None):
                bw = bw or BW
                return [(Wd, K, c0, min(bw, ncols - c0)) for c0 in range(0, ncols, bw)]

            def run_blocks(blist, body):
                names = []
                for (Wd, K, c0, wd) in blist:
                    names.append(None)
                first = True
                nxt = None
                for bi, (Wd, K, c0, wd) in enumerate(blist):
                    if first:
                        k0 = wi[0] % nwb
                        wi[0] += 1
                        load_w(wb[k0], "wb%d" % k0, Wd, K, c0, wd)
                        cur = k0
                        first = False
                    else:
                        cur = nxt
                    if bi + 1 < len(blist):
                        (W2, K2, c2, wd2) = blist[bi + 1]
                        nxt = wi[0] % nwb
                        wi[0] += 1
                        load_w(wb[nxt], "wb%d" % nxt, W2, K2, c2, wd2)
                    body(bi, wb[cur], "wb%d" % cur, K // 128, c0, wd)

            def fm_group(wbuf, wname, KC, col, M, act, aname, tc, kofs=0, tn=512):
                b = nb()
                for kc in range(KC):
                    S.i("pe", "matmul", dict(r=[wname, aname], w=["ps%d" % b]), ps[b][0:M, 0:tn], wbuf[:, kc, col:col + M],
                                                              act[:, kofs + kc, tc * tn:(tc + 1) * tn],
                                                              start=(kc == 0), stop=(kc == KC - 1))
                return b

            def tm_group(wbuf, wname, KC, wd, act, aname, ti, kofs=0):
                b = nb()
                for kc in range(KC):
                    S.i("pe", "matmul", dict(r=[wname, aname], w=["ps%d" % b]), ps[b][:, 0:wd], act[:, kofs + kc, ti * 128:(ti + 1) * 128],
                                                              wbuf[:, kc, 0:wd], start=(kc == 0), stop=(kc == KC - 1))
                return b

            evi = [0]

            def evac(out, b, M=128, wd=512, extra_w=(), extra_r=()):
                evi[0] += 1
                if evi[0] % 2:
                    S.i("act", "copy", dict(r=list(extra_r),
                         w=["ps%d" % b] + list(extra_w)), out=out, in_=ps[b][0:M, 0:wd])
                else:
                    S.i("dve", "tensor_copy", dict(r=list(extra_r),
                         w=["ps%d" % b] + list(extra_w)), out=out, in_=ps[b][0:M, 0:wd])

            return blocks, run_blocks, fm_group, tm_group, evac, si

        agM1 = dtmp("agM1", [256, SEQ], BF16)
        agM1_o = dtmp("agM1_o", [NC * 256, SEQ], BF16)
        agM2 = dtmp("agM2", [SEQ, 256], BF16)
        agM2_o = dtmp("agM2_o", [NC * SEQ, 256], BF16)

        def allgather(src, dst, rname, wname):
            S.op("pool", lambda e: e.collective_compute("AllGather", ALU.bypass, replica_groups=[list(range(NC))],
                                                        ins=[src.ap().opt()], outs=[dst.ap().opt()]),
                 r=[rname], w=[wname], dma=True, cc=True)
        with contextlib.ExitStack() as A:
            xnT = sb(A, "xnT", [128, 32, T], BF16)
            cqn = sb(A, "cqn", [128, 12, T], BF16)
            nwa = sb(A, "nwa", [128, 32])
            nwq = sb(A, "nwq", [128, 12])
            cosT = sb(A, "cosT", [128, T])
            sinT = sb(A, "sinT", [128, T])
            S.dma("sp", nwa[:, :], nw_attn.rearrange("(c p) -> p c", p=128), w=["nwa"], allow_slow_non_contiguous=True)
            S.dma("sp", nwq[:, 0:8], nw_q.rearrange("(c p) -> p c", p=128), w=["nwq"], allow_slow_non_contiguous=True)
            S.dma("sp", nwq[:, 8:12], nw_kv.rearrange("(c p) -> p c", p=128), w=["nwq"], allow_slow_non_contiguous=True)
            with contextlib.ExitStack() as A0:
                posi = sb(A0, "posi", [128, T], I32)
                u = sb(A0, "u", [128, T])
                ui = sb(A0, "ui", [128, T], I32)
                uf = sb(A0, "uf", [128, T])
                S.dma("sp", posi[:, :], pos_d.partition_broadcast(128), w=["posi"])
                S.i("dve", "tensor_copy", dict(r=["posi"], w=["u"]), out=u[:, :], in_=posi[:, :])
                S.i("dve", "tensor_scalar", dict(r=["cst"], w=["u"]), out=u[:, :], in0=u[:, :], scalar1=cst[:, C_MISC:C_MISC + 1],
                                                      scalar2=None, op0=ALU.mult)
                for which, tab in ((0, sinT), (1, cosT)):
                    if which == 1:
                        S.i("dve", "tensor_scalar", dict(w=["u"]), out=u[:, :], in0=u[:, :], scalar1=0.25, scalar2=None,
                                                              op0=ALU.add)
                    S.i("dve", "tensor_copy", dict(r=["u"], w=["ui"]), out=ui[:, :], in_=u[:, :])
                    S.i("dve", "tensor_copy", dict(r=["ui"], w=["uf"]), out=uf[:, :], in_=ui[:, :])
                    S.i("dve", "tensor_tensor", dict(r=["u"], w=["uf"]), out=uf[:, :], in0=u[:, :], in1=uf[:, :], op=ALU.subtract)
                    S.i("dve", "tensor_scalar", dict(r=["uf"], w=["ui"]), out=ui[:, :].bitcast(F32), in0=uf[:, :], scalar1=0.5,
                                                          scalar2=None, op0=ALU.is_gt)
                    S.i("dve", "tensor_tensor", dict(r=["ui"], w=["uf"]), out=uf[:, :], in0=uf[:, :], in1=ui[:, :].bitcast(F32),
                                                          op=ALU.subtract)
                    S.i("dve", "tensor_scalar", dict(r=["uf"], w=["ui"]), out=ui[:, :].bitcast(F32), in0=uf[:, :], scalar1=-0.5,
                                                          scalar2=None, op0=ALU.is_lt)
                    S.i("dve", "tensor_tensor", dict(r=["ui"], w=["uf"]), out=uf[:, :], in0=uf[:, :], in1=ui[:, :].bitcast(F32),
                                                          op=ALU.add)
                    sc = cst[:, C_MISC + 1:C_MISC + 2] if which == 0 else cst[:, C_MISC + 2:C_MISC + 3]
                    S.i("act", "activation", dict(r=["uf", "cst"], w=["tab%d" % which]), out=tab[:, :], in_=uf[:, :], func=AF.Sin, scale=sc)
                S.flush()
            with contextlib.ExitStack() as A1:
                xt = [sb(A1, "xt%d" % i, [128, D]) for i in range(2)]
                xs = [sb(A1, "xs%d" % i, [128, D], BF16) for i in range(2)]
                junk = sb(A1, "junk", [128, D], BF16)
                ssx = sb(A1, "ssx", [128, TT])
                for i in range(TT):
                    b = i % 2
                    S.dma("sp", xt[b][:, :], x_d[i * 128:(i + 1) * 128, :], w=["xt%d" % b])
                    S.i("act", "activation", dict(r=["xt%d" % b], w=["junk", "ssx%d" % i]), out=junk[:, :], in_=xt[b][:, :], func=AF.Square,
                                                                accum_out=ssx[:, i:i + 1])
                    rstd_from_ss("act", ssx[:, i:i + 1], ssx[:, i:i + 1], D, [], ["ssx%d" % i])
                    S.i("dve", "tensor_scalar", dict(r=["xt%d" % b, "ssx%d" % i], w=["xs%d" % b]), out=xs[b][:, :], in0=xt[b][:, :],
                                                                     scalar1=ssx[:, i:i + 1], scalar2=None, op0=ALU.mult)
                    for c8 in range(4):
                        for j in range(8):
                            c = c8 * 8 + j
                            S.i("pe", "transpose", dict(r=["xs%d" % b, "idb"], w=["pst"]), out=pst[:, j * 128:(j + 1) * 128],
                                                                              in_=xs[b][:, c * 128:(c + 1) * 128],
                                                                              identity=idb[:, :])
                        S.i("dve", "tensor_tensor", dict(r=["nwa"], w=["pst", "xnT"]),
                                 out=xnT[:, c8 * 8:(c8 + 1) * 8, i * 128:(i + 1) * 128],
                                 in0=pst[:, :].rearrange("p (c t) -> p c t", c=8),
                                 in1=nwa[:, c8 * 8:(c8 + 1) * 8].unsqueeze(2).to_broadcast([128, 8, 128]), op=ALU.mult)
                S.flush()
            if stage < 1:
                return nc
            with contextlib.ExitStack() as A2:
                wb = [sb(A2, "wb%d" % i, [128, 32, 256], BF16) for i in range(2)]
                stg = [sb(A2, "stg%d" % i, [128, 512]) for i in range(4)]
                stb = [sb(A2, "stb%d" % i, [128, 512], BF16) for i in range(4)]
                blocks, run_blocks, fm_group, tm_group, evac, si = gemm_tools(wb, 256)

                with contextlib.ExitStack() as A3:
                    cT = sb(A3, "cT", [128, 12, T])
                    rs = sb(A3, "rs", [128, 512])

                    def body_c(bi, wbuf, wname, KC, c0, wd):
                        for j in range(wd // 128):
                            gi = c0 // 128 + j
                            for tc in range(TCH):
                                b = fm_group(wbuf, wname, KC, j * 128, 128, xnT, "xnT", tc)
                                evac(cT[:, gi, tc * 512:(tc + 1) * 512], b, extra_w=["cT%d" % gi])
                    run_blocks(blocks(w_cq, D, 1536), body_c)
                    for (g0, ng, n) in ((0, 8, 1024), (8, 4, 512)):
                        for tc in range(TCH):
                            b = nb()
                            for j in range(ng):
                                k = si[0] % 4
                                si[0] += 1
                                S.i("act", "activation", dict(r=["cT%d" % (g0 + j)], w=["stb%d" % k]), out=stb[k][:, :],
                                                                            in_=cT[:, g0 + j, tc * 512:(tc + 1) * 512],
                                                                            func=AF.Square)
                                S.i("pe", "matmul", dict(r=["stb%d" % k, "oneb"], w=["ps%d" % b]), ps[b][:, :], oneb[:, :], stb[k][:, :],
                                                                             start=(j == 0), stop=(j == ng - 1))
                            rstd_from_ss("act", rs[:, :], ps[b][:, :], n, [], ["ps%d" % b, "rs"])
                            for j in range(ng):
                                S.i("dve", "scalar_tensor_tensor", dict(r=["cT%d" % (g0 + j), "nwq", "rs"], w=["cqn"]),
                                    out=cqn[:, g0 + j, tc * 512:(tc + 1) * 512], in0=cT[:, g0 + j, tc * 512:(tc + 1) * 512],
                                    scalar=nwq[:, g0 + j:g0 + j + 1], in1=rs[:, :], op0=ALU.mult, op1=ALU.mult)
                    S.flush()

                def rope_out(bp, br, M, dst_rows, tc):
                    k = si[0] % 4
                    k2 = (si[0] + 1) % 4
                    si[0] += 2
                    S.i("dve", "tensor_tensor", dict(r=["tab1"], w=["ps%d" % bp, "stg%d" % k]), out=stg[k][0:M, :], in0=ps[bp][0:M, :],
                                                          in1=cosT[0:M, tc * 512:(tc + 1) * 512], op=ALU.mult)
                    S.i("dve", "tensor_tensor", dict(r=["tab0"], w=["ps%d" % br, "stg%d" % k2]), out=stg[k2][0:M, :], in0=ps[br][0:M, :],
                                                          in1=sinT[0:M, tc * 512:(tc + 1) * 512], op=ALU.mult)
                    S.i("dve", "tensor_tensor", dict(r=["stg%d" % k2], w=["stg%d" % k, "stb%d" % k]), out=stb[k][0:M, :], in0=stg[k][0:M, :], in1=stg[k2][0:M, :],
                                                          op=ALU.add)
                    S.dma("sp", agA.ap()[dst_rows:dst_rows + M, tc * 512:(tc + 1) * 512], stb[k][0:M, :],
                          r=["stb%d" % k], w=["agA"])

                def body_kpe(bi, wbuf, wname, KC, c0, wd):
                    for tc in range(TCH):
                        bp = fm_group(wbuf, wname, KC, 0, 64, xnT, "xnT", tc)
                        br = fm_group(wbuf, wname, KC, 64, 64, xnT, "xnT", tc)
                        rope_out(bp, br, 64, 5120, tc)
                run_blocks(blocks(w_kpe, D, 128), body_kpe)

                def body_qn(bi, wbuf, wname, KC, c0, wd):
                    for j in range(wd // 128):
                        for tc in range(TCH):
                            b = fm_group(wbuf, wname, KC, j * 128, 128, cqn, "cqn", tc)
                            k = si[0] % 4
                            si[0] += 1
                            evac(stb[k][:, :], b, extra_w=["stb%d" % k])
                            S.dma("sp", agA.ap()[c0 + j * 128:c0 + (j + 1) * 128, tc * 512:(tc + 1) * 512], stb[k][:, :],
                                  r=["stb%d" % k], w=["agA"])
                run_blocks(blocks(w_qn, 1024, 2048), body_qn)
                for half in range(4):
                    load_w(wb[0], "wb0", w_qpe, 1024, half * 256, 256)
                    load_w(wb[1], "wb1", w_qrot, 1024, half * 256, 256)
                    for j in range(2):
                        for tc in range(TCH):
                            bp = fm_group(wb[0], "wb0", 8, j * 128, 128, cqn, "cqn", tc)
                            br = fm_group(wb[1], "wb1", 8, j * 128, 128, cqn, "cqn", tc)
                            rope_out(bp, br, 128, 2048 + half * 256 + j * 128, tc)
                def body_kn(bi, wbuf, wname, KC, c0, wd):
                    for j in range(wd // 128):
                        for tc in range(TCH):
                            b = fm_group(wbuf, wname, KC, j * 128, 128, cqn, "cqn", tc, kofs=8)
                            k = si[0] % 4
                            si[0] += 1
                            evac(stb[k][:, :], b, extra_w=["stb%d" % k])
                            S.dma("sp", agA.ap()[3072 + c0 + j * 128:3072 + c0 + (j + 1) * 128, tc * 512:(tc + 1) * 512],
                                  stb[k][:, :], r=["stb%d" % k], w=["agA"])
                run_blocks(blocks(w_kn, 512, 2048), body_kn)

                def body_v(bi, wbuf, wname, KC, c0, wd):
                    for ti in range(TT):
                        b = tm_group(wbuf, wname, KC, wd, cqn, "cqn", ti, kofs=8)
                        k = si[0] % 4
                        si[0] += 1
                        evac(stb[k][:, 0:wd], b, wd=wd, extra_w=["stb%d" % k])
                        S.dma("sp", agV.ap()[ti * 128:(ti + 1) * 128, c0:c0 + wd], stb[k][:, 0:wd], r=["stb%d" % k], w=["agV"])
                run_blocks(blocks(w_v, 512, 2048), body_v)
                allgather(agA, agA_o, "agA", "agA_o")
                allgather(agV, agV_o, "agV", "agV_o")
                def body_qkv(bi, wbuf, wname, KC, c0, wd):
                    for j in range(wd // 128):
                        for tc in range(TCH):
                            b = fm_group(wbuf, wname, KC, j * 128, 128, xnT, "xnT", tc)
                            k = si[0] % 4
                            si[0] += 1
                            evac(stb[k][:, :], b, extra_w=["stb%d" % k])
                            S.dma("sp", agG.ap()[c0 + j * 128:c0 + (j + 1) * 128, tc * 512:(tc + 1) * 512], stb[k][:, :],
                                  r=["stb%d" % k], w=["agG"])
                run_blocks(blocks(w_qkv, D, 6144), body_qkv)

                def body_gate(bi, wbuf, wname, KC, c0, wd):
                    for ti in range(TT):
                        b = tm_group(wbuf, wname, KC, wd, xnT, "xnT", ti)
                        k = si[0] % 4
                        si[0] += 1
                        S.i("act", "activation", dict(w=["ps%d" % b, "stg%d" % k]), out=stg[k][:, 0:wd], in_=ps[b][:, 0:wd], func=AF.Silu)
                        S.dma("sp", gate_d.ap()[ti * 128:(ti + 1) * 128, c0:c0 + wd], stg[k][:, 0:wd],
                              r=["stg%d" % k], w=["gate_d"])
                run_blocks(blocks(w_gate, D, 2048), body_gate)

                with contextlib.ExitStack() as A5:
                    alb = sb(A5, "alb", [128, 16])
                    dtb = sb(A5, "dtb", [128, 16])
                    ab = sb(A5, "ab", [128, 32])
                    t1 = sb(A5, "t1", [128, 16])
                    t2 = sb(A5, "t2", [128, 16])
                    S.dma("sp", alb[:, :], a_log.partition_broadcast(128), w=["alb"])
                    S.dma("sp", dtb[:, :], dt_bias.partition_broadcast(128), w=["dtb"])
                    S.i("act", "activation", dict(w=["alb"]), out=alb[:, :], in_=alb[:, :], func=AF.Exp)

                    def body_ab(bi, wbuf, wname, KC, c0, wd):
                        for ti in range(TT):
                            b = tm_group(wbuf, wname, KC, 32, xnT, "xnT", ti)
                            S.i("dve", "tensor_tensor", dict(r=["dtb"], w=["ps%d" % b, "t1"]), out=t1[:, :], in0=ps[b][:, 0:16], in1=dtb[:, :],
                                                                       op=ALU.add)
                            S.i("act", "activation", dict(w=["ps%d" % b, "ab"]), out=ab[:, 16:32], in_=ps[b][:, 16:32], func=AF.Sigmoid)
                            S.i("act", "activation", dict(r=["t1"], w=["t2"]), out=t2[:, :], in_=t1[:, :], func=AF.Abs)
                            S.i("act", "activation", dict(w=["t2"]), out=t2[:, :], in_=t2[:, :], func=AF.Exp, scale=-1.0)
                            S.i("act", "activation", dict(w=["t2"]), out=t2[:, :], in_=t2[:, :], func=AF.Ln, bias=1.0)
                            S.i("dve", "scalar_tensor_tensor", dict(r=["t2"], w=["t1"]), out=t1[:, :], in0=t1[:, :], scalar=0.0, in1=t2[:, :],
                                                                         op0=ALU.max, op1=ALU.add)
                            S.i("dve", "scalar_tensor_tensor", dict(r=["t1", "alb"], w=["ab"]), out=ab[:, 0:16], in0=t1[:, :], scalar=-1.0,
                                                                         in1=alb[:, :], op0=ALU.mult, op1=ALU.mult)
                            S.dma("sp", agAB.ap()[ti * 128:(ti + 1) * 128, :], ab[:, :], r=["ab"], w=["agAB"])
                    run_blocks(blocks(w_ab, D, 32), body_ab)
                    S.flush()

                allgather(agG, agG_o, "agG", "agG_o")
                allgather(agAB, agAB_o, "agAB", "agAB_o")
                S.flush()
        if "A" in DEBUG:
            for nm, src in (("agA", agA), ("agV", agV), ("agG", agG), ("agAB", agAB), ("gate", gate_d)):
                a = src.ap()
                n0 = a.shape[0]
                S.dma("sp", dbg[nm].rearrange("(a b) t -> a b t", a=16), a.rearrange("(a b) t -> a b t", a=16),
                      w=["dbg" + nm])
            S.flush()
        if stage < 2:
            return nc
        pid = nc.sync.partition_id()
        if "B1" in DEBUG:
            dbg["m1"] = nc.dram_tensor("dbg_m1", [256, SEQ], BF16, kind="ExternalOutput").ap()

        with contextlib.ExitStack() as B:
            SC = 192.0 ** -0.5
            kpe = sb(B, "kpe", [128, SEQ], BF16)
            cm = [sb(B, "cm%d" % j, [128, 512], BF16) for j in range(4)]
            for j in range(4):
                S.i("dve", "tensor_copy", dict(r=["cst"], w=["cm%d" % j]), out=cm[j][:, :],
                    in_=cst[:, C_CM + 512 * j:C_CM + 512 * (j + 1)])
            Ao = agA_o.ap().rearrange("(r n) t -> n r t", r=NC)
            S.i("dve", "memset", dict(w=["kpe"]), kpe[64:128, :], 0.0)
            S.dma("sp", kpe[0:64, :].rearrange("p (r t) -> p r t", r=NC), Ao[5120:5184, :, :], r=["agA_o"], w=["kpe"])
            qn = [sb(B, "qn%d" % h, [128, SEQ], BF16) for h in range(2)]
            kn = [sb(B, "kn%d" % h, [128, SEQ], BF16) for h in range(2)]
            qpe = [sb(B, "qpe%d" % h, [128, SEQ], BF16) for h in range(2)]
            vv = [sb(B, "vv%d" % h, [128, SEQ // 128, 128], BF16) for h in range(2)]
            pT = [sb(B, "pT%d" % i, [128, 512], BF16) for i in range(3)]
            rinv = sb(B, "rinv", [128, 512])
            ost = [sb(B, "ost%d" % i, [128, 512], BF16) for i in range(2)]
            for h in range(2):
                S.dma("sp", qn[h][:, :].rearrange("p (r t) -> p r t", r=NC), Ao[bass.ds(pid * 256 + h * 128, 128), :, :],
                      r=["agA_o"], w=["qn%d" % h])
                S.dma("sp", kn[h][:, :].rearrange("p (r t) -> p r t", r=NC),
                      Ao[bass.ds(pid * 256 + (3072 + h * 128), 128), :, :], r=["agA_o"], w=["kn%d" % h])
                S.i("dve", "memset", dict(w=["qpe%d" % h]), qpe[h][64:128, :], 0.0)
                S.dma("sp", qpe[h][0:64, :].rearrange("p (r t) -> p r t", r=NC),
                      Ao[bass.ds(pid * 128 + (2048 + h * 64), 64), :, :], r=["agA_o"], w=["qpe%d" % h])
                S.dma("sp", vv[h][:, :, :], agV_o.ap().rearrange("(k p) c -> p k c", p=128)[:, :, bass.ds(pid * 256 + h * 128, 128)],
                      r=["agV_o"], w=["vv%d" % h])
            SB = (0, 1, 2)
            OB = ((3, 4), (5, 6))
            it = [0]
            qi = [0]
            for h in range(2):
                for qt in range(SEQ // 512):
                    nkt = 4 * (qt + 1)
                    ob, lb = OB[qi[0] % 2]
                    qi[0] += 1
                    qs = slice(qt * 512, (qt + 1) * 512)

                    def score(kt):
                        b = SB[(it[0] + kt) % 3]
                        S.i("pe", "matmul", dict(r=["kn%d" % h, "qn%d" % h], w=["ps%d" % b]), ps[b][:, :],
                            kn[h][:, kt * 128:(kt + 1) * 128], qn[h][:, qs], start=True, stop=False)
                        S.i("pe", "matmul", dict(r=["kpe", "qpe%d" % h], w=["ps%d" % b]), ps[b][:, :],
                            kpe[:, kt * 128:(kt + 1) * 128], qpe[h][:, qs], start=False, stop=True)
                        return b
                    bq = [score(0), score(1)]
                    for kt in range(nkt):
                        if kt + 2 < nkt:
                            bq.append(score(kt + 2))
                        bcur = bq[kt]
                        p = pT[(it[0] + kt) % 3]
                        pn = "pT%d" % ((it[0] + kt) % 3)
                        S.i("act", "activation", dict(w=["ps%d" % bcur, pn]), out=p[:, :], in_=ps[bcur][:, :], func=AF.Exp, scale=SC)
                        if kt >= 4 * qt:
                            j = kt - 4 * qt
                            S.i("dve", "tensor_tensor", dict(r=["cm%d" % j], w=[pn]), out=p[:, :], in0=p[:, :], in1=cm[j][:, :], op=ALU.mult)
                        S.i("pe", "matmul", dict(r=["vv%d" % h, pn], w=["ps%d" % ob]), ps[ob][:, :], vv[h][:, kt, :], p[:, :],
                            start=(kt == 0), stop=(kt == nkt - 1))
                        S.i("pe", "matmul", dict(r=["oneb", pn], w=["ps%d" % lb]), ps[lb][:, :], oneb[:, :], p[:, :],
                            start=(kt == 0), stop=(kt == nkt - 1))
                    it[0] += nkt
                    S.i("dve", "reciprocal", dict(w=["ps%d" % lb, "rinv"]), out=rinv[:, :], in_=ps[lb][:, :])
                    o = ost[qi[0] % 2]
                    on = "ost%d" % (qi[0] % 2)
                    S.i("dve", "tensor_tensor", dict(r=["rinv"], w=["ps%d" % ob, on]), out=o[:, :], in0=ps[ob][:, :], in1=rinv[:, :], op=ALU.mult)
                    S.dma("sp", agM1.ap()[h * 128:(h + 1) * 128, qs], o[:, :], r=[on], w=["agM1"])
            S.flush()
        allgather(agM1, agM1_o, "agM1", "agM1_o")
        if "B1" in DEBUG:
            S.dma("sp", dbg["m1"].rearrange("(a b) t -> a b t", a=16), agM1.ap().rearrange("(a b) t -> a b t", a=16), w=["dbgm1"])
            S.flush()
        if stage < 3:
            return nc
        convw_d = din("convw", [128, 24])
        if "B2" in DEBUG:
            dbg["m2"] = nc.dram_tensor("dbg_m2", [SEQ, 256], BF16, kind="ExternalOutput").ap()
        with contextlib.ExitStack() as G:
            NCH = SEQ // 64
            X = [[sb(G, "X%d%d" % (s_, h), [128, 3 + SEQ], BF16) for h in range(2)] for s_ in range(3)]
            cw = sb(G, "cw", [128, 24])
            S.dma("sp", cw[:, :], convw_d, w=["cw"])
            Go = agG_o.ap().rearrange("(r n) t -> n r t", r=NC)
            for s_ in range(3):
                for h in range(2):
                    S.i("dve", "memset", dict(w=["X%d%d" % (s_, h)]), X[s_][h][:, 0:3], 0.0)
                    S.dma("sp", X[s_][h][:, 3:3 + SEQ].rearrange("p (r t) -> p r t", r=NC),
                          Go[bass.ds(pid * 256 + (s_ * 2048 + h * 128), 128), :, :], r=["agG_o"], w=["X%d%d" % (s_, h)])
            with contextlib.ExitStack() as G0:
                acc = [sb(G0, "acc%d" % i, [128, 2048]) for i in range(2)]
                sq = [sb(G0, "sq%d" % i, [128, 2048], BF16) for i in range(2)]
                rn = [sb(G0, "rn%d" % i, [128, 512]) for i in range(2)]
                ci = 0
                for s_ in range(3):
                    for h in range(2):
                        xn_ = "X%d%d" % (s_, h)
                        xv = X[s_][h]
                        for blk in reversed(range(4)):
                            a = acc[ci % 2]
                            an = "acc%d" % (ci % 2)
                            q2 = sq[ci % 2]
                            qn2 = "sq%d" % (ci % 2)
                            ci += 1
                            t0 = 3 + blk * 2048
                            wc = lambda j: cw[:, (s_ * 2 + h) * 4 + j:(s_ * 2 + h) * 4 + j + 1]
                            S.i("dve", "tensor_scalar", dict(r=[xn_, "cw"], w=[an]), out=a[:, :], in0=xv[:, t0:t0 + 2048],
                                scalar1=wc(3), scalar2=None, op0=ALU.mult)
                            for j in (2, 1, 0):
                                sh = 3 - j
                                S.i("dve", "scalar_tensor_tensor", dict(r=[xn_, "cw"], w=[an]), out=a[:, :],
                                    in0=xv[:, t0 - sh:t0 - sh + 2048], scalar=wc(j), in1=a[:, :], op0=ALU.mult, op1=ALU.add)
                            if s_ == 2:
                                S.i("act", "activation", dict(r=[an], w=[xn_]), out=xv[:, t0:t0 + 2048], in_=a[:, :], func=AF.Silu)
                                continue
                            S.i("act", "activation", dict(w=[an]), out=a[:, :], in_=a[:, :], func=AF.Silu)
                            S.i("act", "activation", dict(r=[an], w=[qn2]), out=q2[:, :], in_=a[:, :], func=AF.Square)
                            for c4 in range(4):
                                b = nb()
                                r_ = rn[c4 % 2]
                                rnn = "rn%d" % (c4 % 2)
                                S.i("pe", "matmul", dict(r=[qn2, "oneb"], w=["ps%d" % b]), ps[b][:, :], oneb[:, :],
                                    q2[:, c4 * 512:(c4 + 1) * 512], start=True, stop=True)
                                S.i("act", "activation", dict(r=["cst"], w=["ps%d" % b, rnn]), out=r_[:, :], in_=ps[b][:, :],
                                    func=AF.Sqrt, bias=epsc, scale=1.0)
                                S.i("dve", "reciprocal", dict(w=[rnn]), out=r_[:, :], in_=r_[:, :])
                                S.i("dve", "scalar_tensor_tensor", dict(r=[an, rnn], w=[xn_]),
                                    out=xv[:, t0 + c4 * 512:t0 + (c4 + 1) * 512], in0=a[:, c4 * 512:(c4 + 1) * 512],
                                    scalar=(128.0 ** -0.5 if s_ == 0 else 1.0), in1=r_[:, :], op0=ALU.mult, op1=ALU.mult)
                S.flush()
            gv = [sb(G, "gv%d" % h, [64, NCH]) for h in range(2)]
            bv = [sb(G, "bv%d" % h, [64, NCH]) for h in range(2)]
            gc = [sb(G, "gc%d" % h, [64, NCH]) for h in range(2)]
            egc = [sb(G, "egc%d" % h, [64, NCH]) for h in range(2)]
            etl = [sb(G, "etl%d" % h, [64, NCH]) for h in range(2)]
            bgc = [sb(G, "bgc%d" % h, [64, NCH]) for h in range(2)]
            cd = [sb(G, "cd%d" % h, [128, NCH]) for h in range(2)]
            Sf = [sb(G, "Sf%d" % h, [128, 128]) for h in range(2)]
            Sb = [sb(G, "Sb%d" % h, [128, 128], BF16) for h in range(2)]
            ABo = agAB_o.ap().rearrange("(n i) c -> i n c", i=64)
            U64 = cst[0:64, C_U:C_U + 64]
            ones64 = cst[0:64, C_ONE:C_ONE + 64]
            ones64x128 = cst[0:64, C_ONE:C_ONE + 128]
            id64f = cst[0:64, C_ID:C_ID + 64]
            for h in range(2):
                S.dma("sp", gv[h][:, :].unsqueeze(2), ABo[:, :, bass.ds(pid * 2 + h, 1)], r=["agAB_o"], w=["gv%d" % h],
                      allow_slow_non_contiguous=True)
                S.dma("sp", bv[h][:, :].unsqueeze(2), ABo[:, :, bass.ds(pid * 2 + (16 + h), 1)], r=["agAB_o"], w=["bv%d" % h],
                      allow_slow_non_contiguous=True)
                b1 = nb()
                S.i("pe", "matmul", dict(r=["cst", "gv%d" % h], w=["ps%d" % b1]), ps[b1][0:64, 0:NCH], U64, gv[h][:, :], start=True, stop=True)
                S.i("pe", "matmul", dict(r=["cst", "gv%d" % h], w=["ps%d" % b1]), ps[b1][0:64, 128:128 + NCH], ones64, gv[h][:, :], start=True, stop=True)
                S.i("pe", "matmul", dict(r=["cst", "gv%d" % h], w=["ps%d" % b1]), ps[b1][:, 256:256 + NCH], ones64x128, gv[h][:, :], start=True, stop=True)
                S.i("dve", "tensor_copy", dict(w=["ps%d" % b1, "gc%d" % h]), out=gc[h][:, :], in_=ps[b1][0:64, 0:NCH])
                S.i("dve", "tensor_tensor", dict(r=["gc%d" % h], w=["ps%d" % b1, "etl%d" % h]), out=etl[h][:, :],
                    in0=ps[b1][0:64, 128:128 + NCH], in1=gc[h][:, :], op=ALU.subtract)
                S.i("act", "activation", dict(w=["etl%d" % h]), out=etl[h][:, :], in_=etl[h][:, :], func=AF.Exp)
                S.i("act", "activation", dict(r=["gc%d" % h], w=["egc%d" % h]), out=egc[h][:, :], in_=gc[h][:, :], func=AF.Exp)
                S.i("act", "activation", dict(w=["ps%d" % b1, "cd%d" % h]), out=cd[h][:, :], in_=ps[b1][:, 256:256 + NCH], func=AF.Exp)
                S.i("dve", "tensor_tensor", dict(r=["egc%d" % h, "bv%d" % h], w=["bgc%d" % h]), out=bgc[h][:, :], in0=bv[h][:, :],
                    in1=egc[h][:, :], op=ALU.mult)
                S.i("dve", "memset", dict(w=["Sf%d" % h]), Sf[h][:, :], 0.0)
                S.i("dve", "memset", dict(w=["Sb%d" % h]), Sb[h][:, :], 0.0)
            def mk(name, shape, dt=BF16, n=2):
                return [[sb(G, "%s%d%d" % (name, h, i), shape, dt) for i in range(n)] for h in range(2)]
            kbg = mk("kbg", [64, 8, 128])
            ktl = mk("ktl", [64, 8, 128])
            vb = mk("vb", [64, 8, 128])
            qkT = mk("qkT", [64, 8, 64])
            uu = mk("uu", [64, 8, 128], F32)
            wT = mk("wT", [128, 8, 64])
            ot = mk("ot", [64, 8, 128], BF16)
            Am = mk("Am", [64, 8, 64], BF16, 1)
            Bm = mk("Bm", [64, 8, 64], BF16, 1)
            PA = mk("PA", [64, 8, 64], BF16, 2)
            PB = mk("PB", [64, 8, 64], BF16, 2)
            Rm = mk("Rm", [64, 8, 64], BF16, 1)
            DS = mk("DS", [64, 8, 64], F32, 1)
            DT = mk("DT", [64, 8, 64], F32, 1)
            Rd = DT
            vn = mk("vn", [64, 128], BF16, 2)
            o1 = mk("o1", [64, 128], F32, 2)
            pbk = [3]

            def pbank():
                pbk[0] = 4 + (pbk[0] - 3) % 3
                return pbk[0]
            maskT = cst[0:64, C_MT:C_MT + 64].unsqueeze(1).to_broadcast([64, 8, 64])
            maskS = cst[0:64, C_MS:C_MS + 64].unsqueeze(1).to_broadcast([64, 8, 64])

            def v3(ap):
                return ap.rearrange("p (c i) -> p c i", c=8)

            def stages(bi):
                n0 = bi * 8
                pb = bi % 2
                tb = 3 + n0 * 64
                R = lambda h, nm: "%s%d%d" % (nm, h, pb)
                R1 = lambda h, nm: "%s%d0" % (nm, h)

                def bc(vec, h, wdt):
                    return vec[h][:, n0:n0 + 8].unsqueeze(2).to_broadcast([64, 8, wdt])

                def s0():
                    for h in range(2):
                        for c in range(8):
                            S.i("pe", "transpose", dict(r=["X1%d" % h, "idb"], w=["pst"]), out=pst[0:64, c * 128:(c + 1) * 128],
                                in_=X[1][h][:, tb + c * 64:tb + (c + 1) * 64], identity=idb[:, :])
                        pv = pst[0:64, :].rearrange("p (c d) -> p c d", c=8)
                        S.i("dve", "tensor_tensor", dict(r=["bgc%d" % h], w=["pst", R(h, "kbg")]), out=kbg[h][pb][:, :, :], in0=pv,
                            in1=bc(bgc, h, 128), op=ALU.mult)
                        S.i("dve", "tensor_tensor", dict(r=["etl%d" % h], w=["pst", R(h, "ktl")]), out=ktl[h][pb][:, :, :], in0=pv,
                            in1=bc(etl, h, 128), op=ALU.mult)

                def s1():
                    for h in range(2):
                        for c in range(8):
                            S.i("pe", "transpose", dict(r=["X2%d" % h, "idb"], w=["pst"]), out=pst[0:64, c * 128:(c + 1) * 128],
                                in_=X[2][h][:, tb + c * 64:tb + (c + 1) * 64], identity=idb[:, :])
                        pv = pst[0:64, :].rearrange("p (c d) -> p c d", c=8)
                        S.i("dve", "tensor_tensor", dict(r=["bv%d" % h], w=["pst", R(h, "vb")]), out=vb[h][pb][:, :, :], in0=pv,
                            in1=bc(bv, h, 128), op=ALU.mult)
                        S.i("dve", "tensor_tensor", dict(r=["gc%d" % h, "cst"], w=[R1(h, "DT")]), out=Rd[h][0][:, :, :], in0=bc(gc, h, 64),
                            in1=id64f.unsqueeze(1).to_broadcast([64, 8, 64]), op=ALU.mult)
                        b = pbank()
                        S.i("pe", "matmul", dict(r=[R1(h, "DT"), "cst"], w=["ps%d" % b]), ps[b][0:64, :], ones64,
                            Rd[h][0][:, :, :].rearrange("p c i -> p (c i)"), start=True, stop=True)
                        g3 = v3(ps[b][0:64, :])
                        S.i("dve", "tensor_tensor", dict(r=["gc%d" % h], w=["ps%d" % b, R1(h, "DT")]), out=DT[h][0][:, :, :], in0=g3,
                            in1=bc(gc, h, 64), op=ALU.subtract)
                        S.i("dve", "tensor_tensor", dict(r=["gc%d" % h], w=["ps%d" % b, R1(h, "DS")]), out=DS[h][0][:, :, :],
                            in0=bc(gc, h, 64), in1=g3, op=ALU.subtract)
                        S.i("dve", "tensor_tensor", dict(r=["cst"], w=[R1(h, "DT")]), out=DT[h][0][:, :, :], in0=DT[h][0][:, :, :],
                            in1=maskT, op=ALU.add)
                        S.i("dve", "tensor_tensor", dict(r=["cst"], w=[R1(h, "DS")]), out=DS[h][0][:, :, :], in0=DS[h][0][:, :, :],
                            in1=maskS, op=ALU.add)
                        S.i("act", "activation", dict(w=[R1(h, "DT")]), out=DT[h][0][:, :, :], in_=DT[h][0][:, :, :], func=AF.Exp)
                        S.i("act", "activation", dict(w=[R1(h, "DS")]), out=DS[h][0][:, :, :], in_=DS[h][0][:, :, :], func=AF.Exp)

                def s2():
                    for h in range(2):
                        b = pbank()
                        for c in range(8):
                            kc_ = X[1][h][:, tb + c * 64:tb + (c + 1) * 64]
                            S.i("pe", "matmul", dict(r=["X1%d" % h], w=["ps%d" % b]), ps[b][0:64, c * 64:(c + 1) * 64], kc_, kc_,
                                start=True, stop=True)
                        S.i("dve", "tensor_tensor", dict(r=[R1(h, "DS")], w=["ps%d" % b, R1(h, "Am")]), out=DS[h][0][:, :, :],
                            in0=v3(ps[b][0:64, :]), in1=DS[h][0][:, :, :], op=ALU.mult)
                        S.i("dve", "tensor_tensor", dict(r=["bv%d" % h, R1(h, "DS")], w=[R1(h, "Am")]), out=Am[h][0][:, :, :],
                            in0=DS[h][0][:, :, :], in1=bc(bv, h, 64), op=ALU.mult)
                        b = pbank()
                        for c in range(8):
                            S.i("pe", "matmul", dict(r=["X1%d" % h, "X0%d" % h], w=["ps%d" % b]), ps[b][0:64, c * 64:(c + 1) * 64],
                                X[1][h][:, tb + c * 64:tb + (c + 1) * 64], X[0][h][:, tb + c * 64:tb + (c + 1) * 64],
                                start=True, stop=True)
                        S.i("dve", "tensor_tensor", dict(r=[R1(h, "DT")], w=["ps%d" % b, R(h, "qkT")]), out=qkT[h][pb][:, :, :],
                            in0=v3(ps[b][0:64, :]), in1=DT[h][0][:, :, :], op=ALU.mult)

                def s3():
                    for h in range(2):
                        for c in range(8):
                            S.i("pe", "transpose", dict(r=[R1(h, "Am"), "idb"], w=["pst"]), out=pst[0:64, c * 64:(c + 1) * 64],
                                in_=Am[h][0][:, c, :], identity=idb[0:64, 0:64])
                        pv = pst[0:64, 0:512].rearrange("p (c d) -> p c d", c=8)
                        S.i("act", "copy", dict(w=["pst", R1(h, "Bm")]), out=Bm[h][0][:, :, :], in_=pv)
                        S.i("dve", "tensor_tensor", dict(r=["idb"], w=["pst", R1(h, "Rm")]), out=Rm[h][0][:, :, :],
                            in0=idb[0:64, 0:64].unsqueeze(1).to_broadcast([64, 8, 64]), in1=pv, op=ALU.subtract)

                def level(lv):
                    def f():
                        for h in range(2):
                            pa_in = Am[h][0] if lv == 1 else PA[h][(lv - 1) % 2]
                            pb_in = Bm[h][0] if lv == 1 else PB[h][(lv - 1) % 2]
                            pan = R1(h, "Am") if lv == 1 else "PA%d%d" % (h, (lv - 1) % 2)
                            pbn = R1(h, "Bm") if lv == 1 else "PB%d%d" % (h, (lv - 1) % 2)
                            pa_o = PA[h][lv % 2]
                            pb_o = PB[h][lv % 2]
                            pao = "PA%d%d" % (h, lv % 2)
                            pbo = "PB%d%d" % (h, lv % 2)
                            b = pbank()
                            for c in range(8):
                                S.i("pe", "matmul", dict(r=[pan, pbn], w=["ps%d" % b]), ps[b][0:64, c * 64:(c + 1) * 64],
                                    pb_in[:, c, :], pa_in[:, c, :], start=True, stop=True)
                            S.i("act", "copy", dict(w=["ps%d" % b, pao]), out=pa_o[:, :, :], in_=v3(ps[b][0:64, :]))
                            if lv < 5:
                                b = pbank()
                                for c in range(8):
                                    S.i("pe", "matmul", dict(r=[pan, pbn], w=["ps%d" % b]), ps[b][0:64, c * 64:(c + 1) * 64],
                                        pa_in[:, c, :], pb_in[:, c, :], start=True, stop=True)
                                S.i("act", "copy", dict(w=["ps%d" % b, pbo]), out=pb_o[:, :, :], in_=v3(ps[b][0:64, :]))
                            b = pbank()
                            for c in range(8):
                                S.i("pe", "matmul", dict(r=[pao, R1(h, "Rm")], w=["ps%d" % b]), ps[b][0:64, c * 64:(c + 1) * 64],
                                    pa_o[:, c, :], Rm[h][0][:, c, :], start=True, stop=True)
                            S.i("dve", "tensor_tensor", dict(w=["ps%d" % b, R1(h, "Rm")]), out=Rm[h][0][:, :, :],
                                in0=v3(ps[b][0:64, :]), in1=Rm[h][0][:, :, :], op=ALU.add)
                    return f

                def s7():
                    for h in range(2):
                        for half in range(2):
                            b = pbank()
                            for c4 in range(4):
                                c = half * 4 + c4
                                S.i("pe", "matmul", dict(r=[R1(h, "Rm"), R(h, "vb")], w=["ps%d" % b]),
                                    ps[b][0:64, c4 * 128:(c4 + 1) * 128], Rm[h][0][:, c, :], vb[h][pb][:, c, :], start=True, stop=True)
                            S.i("act", "copy", dict(w=["ps%d" % b, R(h, "uu")]), out=uu[h][pb][:, half * 4:(half + 1) * 4, :],
                                in_=ps[b][0:64, :].rearrange("p (c d) -> p c d", c=4))
                        b = pbank()
                        for c in range(8):
                            S.i("pe", "matmul", dict(r=[R1(h, "Rm"), R(h, "kbg")], w=["ps%d" % b]), ps[b][:, c * 64:(c + 1) * 64],
                                kbg[h][pb][:, c, :], Rm[h][0][:, c, :], start=True, stop=True)
                        S.i("dve", "tensor_copy", dict(w=["ps%d" % b, R(h, "wT")]), out=wT[h][pb][:, :, :],
                            in_=ps[b][:, :].rearrange("p (c i) -> p c i", c=8))

                def l23():
                    level(2)()
                    level(3)()

                def l45():
                    level(4)()
                    level(5)()
                return [s0, s1, s2, s3, level(1), l23, l45, s7]

            def seq_step(bi, c):
                n = bi * 8 + c
                pb = bi % 2
                tq = 3 + n * 64
                for h in range(2):
                    ba, bb = (0, 1) if h == 0 else (2, 3)
                    v_ = vn[h][c % 2]
                    vnn = "vn%d%d" % (h, c % 2)
                    o_ = o1[h][c % 2]
                    o1n = "o1%d%d" % (h, c % 2)
                    sbn = "Sb%d" % h
                    S.i("pe", "matmul", dict(r=["wT%d%d" % (h, pb), sbn], w=["ps%d" % ba]), ps[ba][0:64, 0:128], wT[h][pb][:, c, :],
                        Sb[h][:, :], start=True, stop=True)
                    S.i("pe", "matmul", dict(r=["X0%d" % h, sbn], w=["ps%d" % ba]), ps[ba][0:64, 128:256],
                        X[0][h][:, tq:tq + 64], Sb[h][:, :], start=True, stop=True)
                    S.i("dve", "tensor_tensor", dict(r=["uu%d%d" % (h, pb)], w=["ps%d" % ba, vnn]), out=v_[:, :], in0=uu[h][pb][:, c, :],
                        in1=ps[ba][0:64, 0:128], op=ALU.subtract)
                    S.i("act", "activation", dict(r=["egc%d" % h], w=["ps%d" % ba, o1n]), out=o_[:, :], in_=ps[ba][0:64, 128:256],
                        func=AF.Identity, scale=egc[h][:, n:n + 1])
                    S.i("pe", "matmul", dict(r=["qkT%d%d" % (h, pb), vnn], w=["ps%d" % bb]), ps[bb][0:64, 0:128], qkT[h][pb][:, c, :],
                        v_[:, :], start=True, stop=True)
                    S.i("pe", "matmul", dict(r=["ktl%d%d" % (h, pb), vnn], w=["ps%d" % bb]), ps[bb][:, 128:256], ktl[h][pb][:, c, :],
                        v_[:, :], start=True, stop=True)
                    S.i("dve", "tensor_tensor", dict(r=[o1n], w=["ps%d" % bb, "ot%d%d" % (h, pb)]), out=ot[h][pb][:, c, :],
                        in0=ps[bb][0:64, 0:128], in1=o_[:, :], op=ALU.add)
                    S.i("dve", "scalar_tensor_tensor", dict(r=["cd%d" % h], w=["ps%d" % bb, "Sf%d" % h]), out=Sf[h][:, :], in0=Sf[h][:, :],
                        scalar=cd[h][:, n:n + 1], in1=ps[bb][:, 128:256], op0=ALU.mult, op1=ALU.add)
                    S.i("act", "copy", dict(r=["Sf%d" % h], w=[sbn]), out=Sb[h][:, :], in_=Sf[h][:, :])
                    if c == 7:
                        S.dma("sp", agM2.ap()[bi * 512:(bi + 1) * 512, h * 128:(h + 1) * 128].rearrange("(c i) d -> i c d", i=64),
                              ot[h][pb][:, :, :], r=["ot%d%d" % (h, pb)], w=["agM2"])

            NB = NCH // 8
            gd_w = dtmp("gd_w", [2 * 16 * 128, 512], BF16).ap().rearrange("(h b p) n -> h b p n", h=2, b=16)
            gd_u = dtmp("gd_u", [2 * 16 * 64, 1024], F32).ap().rearrange("(h b p) n -> h b p n", h=2, b=16)
            gd_q = dtmp("gd_q", [2 * 16 * 64, 512], BF16).ap().rearrange("(h b p) n -> h b p n", h=2, b=16)
            gd_k = dtmp("gd_k", [2 * 16 * 64, 1024], BF16).ap().rearrange("(h b p) n -> h b p n", h=2, b=16)
            fl = lambda t: t[:, :, :].rearrange("p c n -> p (c n)")
            for bi in range(NB):
                for f in stages(bi):
                    f()
                pb = bi % 2
                for h in range(2):
                    S.dma("sp", gd_w[h, bi], fl(wT[h][pb]), r=["wT%d%d" % (h, pb)], w=["gd_w%d_%d" % (h, bi)])
                    S.dma("sp", gd_u[h, bi], fl(uu[h][pb]), r=["uu%d%d" % (h, pb)], w=["gd_u%d_%d" % (h, bi)])
                    S.dma("sp", gd_q[h, bi], fl(qkT[h][pb]), r=["qkT%d%d" % (h, pb)], w=["gd_q%d_%d" % (h, bi)])
                    S.dma("sp", gd_k[h, bi], fl(ktl[h][pb]), r=["ktl%d%d" % (h, pb)], w=["gd_k%d_%d" % (h, bi)])

            def load(bi):
                pb = bi % 2
                for h in range(2):
                    S.dma("sp", fl(wT[h][pb]), gd_w[h, bi], r=["gd_w%d_%d" % (h, bi)], w=["wT%d%d" % (h, pb)])
                    S.dma("sp", fl(uu[h][pb]), gd_u[h, bi], r=["gd_u%d_%d" % (h, bi)], w=["uu%d%d" % (h, pb)])
                    S.dma("sp", fl(qkT[h][pb]), gd_q[h, bi], r=["gd_q%d_%d" % (h, bi)], w=["qkT%d%d" % (h, pb)])
                    S.dma("sp", fl(ktl[h][pb]), gd_k[h, bi], r=["gd_k%d_%d" % (h, bi)], w=["ktl%d%d" % (h, pb)])
            load(0)
            for bi in range(NB):
                if bi + 1 < NB:
                    load(bi + 1)
                for c in range(8):
                    seq_step(bi, c)
            S.flush()
        if "B2" in DEBUG:
            S.dma("sp", dbg["m2"].rearrange("(a b) t -> a b t", a=16), agM2.ap().rearrange("(a b) t -> a b t", a=16), w=["dbgm2"])
            S.flush()
        if stage < 4:
            return nc
        mem_d = din("mem", [256, D])
        gnw_d = din("gnw", [1, 128])
        w_out_d = din("w_out", [D, D])
        nw_x = din("nw_x", [D])
        nw_m = din("nw_m", [D])
        wq_d = din("wxq", [D, D])
        wk_d = din("wxk", [D, D])
        wv_d = din("wxv", [D, D])
        wo_d = din("wxo", [D, D])
        nw_mlp = din("nw_mlp", [D])
        w_up_d = din("w_up", [D, 4 * D])
        w_dn_d = din("w_dn", [4 * D, D])
        nw_f = din("nw_f", [1, D])
        out_d = nc.dram_tensor("out", [T, D], F32, kind="ExternalOutput").ap()
        h_d = dtmp("h_d", [T, D], F32).ap()
        u_d = dtmp("u_d", [4 * D, T], BF16).ap()
        allgather(agM2, agM2_o, "agM2", "agM2_o")

        def norm_Tg(src, ntiles, nw_dram, dst, dname, scr, nss, nwt):
                xt = [scr[:, 0:8192].bitcast(F32), scr[:, 8192:16384].bitcast(F32)]
                xs = [scr[:, 16384:20480], scr[:, 20480:24576]]
                S.dma("sp", nwt[:, :], nw_dram.rearrange("(c p) -> p c", p=128), w=["nwt"], allow_slow_non_contiguous=True)
                for i in range(ntiles):
                    b = i % 2
                    S.dma("sp", xt[b][:, :], src[i * 128:(i + 1) * 128, :], w=["nxt%d" % b])
                    S.i("act", "activation", dict(r=["nxt%d" % b], w=["nxs%d" % b, "nss%d" % i]), out=xs[b][:, :], in_=xt[b][:, :],
                        func=AF.Square, accum_out=nss[:, i:i + 1])
                    rstd_from_ss("act", nss[:, i:i + 1], nss[:, i:i + 1], D, [], ["nss%d" % i])
                    S.i("dve", "tensor_scalar", dict(r=["nxt%d" % b, "nss%d" % i], w=["nxs%d" % b]), out=xs[b][:, :], in0=xt[b][:, :],
                        scalar1=nss[:, i:i + 1], scalar2=None, op0=ALU.mult)
                    for c8 in range(4):
                        for j in range(8):
                            c = c8 * 8 + j
                            S.i("pe", "transpose", dict(r=["nxs%d" % b, "idb"], w=["pst"]), out=pst[:, j * 128:(j + 1) * 128],
                                in_=xs[b][:, c * 128:(c + 1) * 128], identity=idb[:, :])
                        S.i("dve", "tensor_tensor", dict(r=["nwt"], w=["pst", dname]), out=dst[:, c8 * 8:(c8 + 1) * 8, i * 128:(i + 1) * 128],
                            in0=pst[:, :].rearrange("p (c t) -> p c t", c=8),
                            in1=nwt[:, c8 * 8:(c8 + 1) * 8].unsqueeze(2).to_broadcast([128, 8, 128]), op=ALU.mult)
                S.flush()


        kx_d = dtmp("kx_d", [128, 32 * 256], BF16).ap()
        vx_d = dtmp("vx_d", [128, 2 * D], BF16).ap()
        with contextlib.ExitStack() as P0:
            pscr = sb(P0, "pscr", [128, 24576], BF16)
            pnss = sb(P0, "pnss", [128, 8])
            pnwt = sb(P0, "pnwt", [128, 32])
            memT = sb(P0, "memT", [128, 32, 256], BF16)
            kxT = sb(P0, "kxT", [128, 32, 256], BF16)
            vx = sb(P0, "vx", [128, 2, D], BF16)
            pwb = [sb(P0, "pwb%d" % i, [128, 32, 256], BF16) for i in range(3)]
            norm_Tg(mem_d, 2, nw_m, memT, "memT", pscr, pnss, pnwt)
            blocksB, run_blocksB, fm_groupB, tm_groupB, evacB, siB = gemm_tools(pwb, 256)

            def body_k(bi, wbuf, wname, KC, c0, wd):
                for j in range(wd // 128):
                    b = fm_groupB(wbuf, wname, KC, j * 128, 128, memT, "memT", 0, tn=256)
                    evacB(kxT[:, c0 // 128 + j, :], b, wd=256, extra_w=["kxT"])
            run_blocksB(blocksB(wk_d, D, D), body_k)

            def body_v(bi, wbuf, wname, KC, c0, wd):
                for ti in range(2):
                    b = tm_groupB(wbuf, wname, KC, wd, memT, "memT", ti)
                    evacB(vx[:, ti, c0:c0 + wd], b, wd=wd, extra_w=["vx"])
            run_blocksB(blocksB(wv_d, D, D), body_v)
            S.dma("sp", kx_d, kxT[:, :, :].rearrange("p c n -> p (c n)"), r=["kxT"], w=["kx_d"])
            S.dma("sp", vx_d, vx[:, :, :].rearrange("p k n -> p (k n)"), r=["vx"], w=["vx_d"])
            S.flush()
        with contextlib.ExitStack() as Cc:
            actA = sb(Cc, "actA", [128, 32, T], BF16)
            actB = sb(Cc, "actB", [128, 32, T], BF16)
            pida = nc.scalar.partition_id()
            S.dma("act", actA[:, 0:16, :], agM1_o.ap().rearrange("(k p) t -> p k t", p=128)[:, :, bass.ds(pida * T, T)],
                  r=["agM1_o"], w=["actA"])
            with contextlib.ExitStack() as C1:
                go = [sb(C1, "go%d" % i, [128, 16, 128], BF16) for i in range(2)]
                gt = [sb(C1, "gt%d" % i, [128, 16, 128]) for i in range(2)]
                gsq = sb(C1, "gsq", [128, 16, 128])
                gb = [sb(C1, "gb%d" % i, [128, 16, 128], BF16) for i in range(2)]
                gss = sb(C1, "gss", [128, 16])
                gnw = sb(C1, "gnw", [128, 128])
                S.dma("sp", gnw[:, :], gnw_d.partition_broadcast(128), w=["gnw"])
                M2o = agM2_o.ap().rearrange("(r t) c -> t r c", r=NC)
                for i in range(TT):
                    b = i % 2
                    S.dma("act", go[b][:, :, :].rearrange("p (r h) d -> p r (h d)", r=NC), M2o[bass.ds(pida * T + i * 128, 128), :, :],
                          r=["agM2_o"], w=["go%d" % b])
                    S.dma("sp", gt[b][:, :, :].rearrange("p h d -> p (h d)"), gate_d.ap()[i * 128:(i + 1) * 128, :], w=["gt%d" % b])
                    S.i("dve", "tensor_tensor", dict(r=["go%d" % b], w=["gsq"]), out=gsq[:, :, :], in0=go[b][:, :, :], in1=go[b][:, :, :], op=ALU.mult)
                    S.i("dve", "tensor_reduce", dict(r=["gsq"], w=["gss"]), out=gss[:, :], in_=gsq[:, :, :], axis=AX.X, op=ALU.add)
                    rstd_from_ss("act", gss[:, :], gss[:, :], 128, [], ["gss"])
                    S.i("dve", "tensor_tensor", dict(r=["gss", "go%d" % b], w=["gsq"]), out=gsq[:, :, :], in0=go[b][:, :, :],
                        in1=gss[:, :].unsqueeze(2).to_broadcast([128, 16, 128]), op=ALU.mult)
                    S.i("dve", "tensor_tensor", dict(r=["gnw"], w=["gsq"]), out=gsq[:, :, :], in0=gsq[:, :, :],
                        in1=gnw[:, :].unsqueeze(1).to_broadcast([128, 16, 128]), op=ALU.mult)
                    S.i("dve", "tensor_tensor", dict(r=["gt%d" % b, "gsq"], w=["gb%d" % b]), out=gb[b][:, :, :], in0=gsq[:, :, :],
                        in1=gt[b][:, :, :], op=ALU.mult)
                    for c8 in range(2):
                        for j in range(8):
                            S.i("pe", "transpose", dict(r=["gb%d" % b, "idb"], w=["pst"]), out=pst[:, j * 128:(j + 1) * 128],
                                in_=gb[b][:, c8 * 8 + j, :], identity=idb[:, :])
                        S.i("act", "copy", dict(w=["pst", "actA"]), out=actA[:, 16 + c8 * 8:16 + (c8 + 1) * 8, i * 128:(i + 1) * 128],
                            in_=pst[:, :].rearrange("p (c t) -> p c t", c=8))
                S.flush()
            scr = sb(Cc, "scr", [128, 32768], BF16)
            nss = sb(Cc, "nss", [128, 8])
            nwt = sb(Cc, "nwt", [128, 32])
            wb = [scr[:, i * 8192:(i + 1) * 8192].rearrange("p (c n) -> p c n", c=32) for i in range(3)]
            stg = [scr[:, 24576 + i * 1024:24576 + (i + 1) * 1024].bitcast(F32) for i in range(4)]
            stb = [scr[:, 28672 + i * 512:28672 + (i + 1) * 512] for i in range(4)]
            xr = [scr[:, 30720 + i * 512:30720 + (i + 1) * 512].bitcast(F32) for i in range(4)]
            blocks, run_blocks, fm_group, tm_group, evac, si = gemm_tools(wb, 256)
            xi = [0]

            def norm_T(src, ntiles, nw_dram, dst, dname):
                norm_Tg(src, ntiles, nw_dram, dst, dname, scr, nss, nwt)

            def resid_gemm(Wd, K, act, aname, res_src, dst):
                def body(bi, wbuf, wname, KC, c0, wd):
                    for ti in range(TT):
                        b = tm_group(wbuf, wname, KC, wd, act, aname, ti)
                        k = xi[0] % 4
                        xi[0] += 1
                        S.dma("sp", xr[k][:, 0:wd], res_src[ti * 128:(ti + 1) * 128, c0:c0 + wd], r=["hd%d_%d" % (ti, c0)],
                              w=["cxr%d" % k])
                        S.i("dve", "tensor_tensor", dict(w=["ps%d" % b, "cxr%d" % k]), out=xr[k][:, 0:wd], in0=ps[b][:, 0:wd],
                            in1=xr[k][:, 0:wd], op=ALU.add)
                        S.dma("sp", dst[ti * 128:(ti + 1) * 128, c0:c0 + wd], xr[k][:, 0:wd], r=["cxr%d" % k],
                              w=["hd%d_%d" % (ti, c0)])
                run_blocks(blocks(Wd, K, D), body)
                S.flush()

            resid_gemm(w_out_d, D, actA, "actA", x_d, h_d)
            norm_T(h_d, TT, nw_x, actB, "actB")

            def body_q(bi, wbuf, wname, KC, c0, wd):
                for j in range(wd // 128):
                    for tc in range(TCH):
                        b = fm_group(wbuf, wname, KC, j * 128, 128, actB, "actB", tc)
                        evac(actA[:, c0 // 128 + j, tc * 512:(tc + 1) * 512], b, extra_w=["actA"])
            run_blocks(blocks(wq_d, D, D), body_q)
            S.flush()
            memT = scr[:, 24576:32768].rearrange("p (c n) -> p c n", c=32)
            kxT = scr[:, 0:8192].rearrange("p (c n) -> p c n", c=32)
            vx = scr[:, 8192:16384].rearrange("p (k n) -> p k n", k=2)
            pT2 = [scr[:, 16384 + i * 512:16384 + (i + 1) * 512] for i in range(2)]
            rv2 = scr[:, 17408:18432].bitcast(F32)
            S.dma("sp", scr[:, 0:8192], kx_d, w=["kxT"])
            S.dma("sp", scr[:, 8192:16384], vx_d, w=["vx"])
            SCX = 1024.0 ** -0.5
            for hd in range(4):
                for tc in range(TCH):
                    ts_ = slice(tc * 512, (tc + 1) * 512)
                    lb = nb()
                    for kt in range(2):
                        b = nb()
                        for dc in range(8):
                            S.i("pe", "matmul", dict(r=["kxT", "actA"], w=["ps%d" % b]), ps[b][:, :],
                                kxT[:, hd * 8 + dc, kt * 128:(kt + 1) * 128], actA[:, hd * 8 + dc, ts_], start=(dc == 0), stop=(dc == 7))
                        S.i("act", "activation", dict(w=["ps%d" % b, "pT2%d" % kt]), out=pT2[kt][:, :], in_=ps[b][:, :], func=AF.Exp, scale=SCX)
                        S.i("pe", "matmul", dict(r=["oneb", "pT2%d" % kt], w=["ps%d" % lb]), ps[lb][:, :], oneb[:, :], pT2[kt][:, :],
                            start=(kt == 0), stop=(kt == 1))
                    S.i("dve", "reciprocal", dict(w=["ps%d" % lb, "rv2"]), out=rv2[:, :], in_=ps[lb][:, :])
                    for dvc in range(8):
                        b = nb()
                        for kt in range(2):
                            S.i("pe", "matmul", dict(r=["vx", "pT2%d" % kt], w=["ps%d" % b]), ps[b][:, :],
                                vx[:, kt, hd * 1024 + dvc * 128:hd * 1024 + (dvc + 1) * 128], pT2[kt][:, :], start=(kt == 0), stop=(kt == 1))
                        S.i("dve", "tensor_tensor", dict(r=["rv2"], w=["ps%d" % b, "actB"]), out=actB[:, hd * 8 + dvc, ts_],
                            in0=ps[b][:, :], in1=rv2[:, :], op=ALU.mult)
            S.flush()
            resid_gemm(wo_d, D, actB, "actB", h_d, h_d)
            norm_T(h_d, TT, nw_mlp, actA, "actA")

            def body_up(bi, wbuf, wname, KC, c0, wd):
                for j in range(wd // 128):
                    for tc in range(TCH):
                        b = fm_group(wbuf, wname, KC, j * 128, 128, actA, "actA", tc)
                        k = si[0] % 4
                        si[0] += 1
                        S.i("act", "activation", dict(w=["ps%d" % b, "cstg%d" % k]), out=stg[k][:, :], in_=ps[b][:, :], func=AF.Relu)
                        S.i("dve", "tensor_tensor", dict(r=["cstg%d" % k], w=["cstb%d" % k]), out=stb[k][:, :], in0=stg[k][:, :], in1=stg[k][:, :],
                            op=ALU.mult)
                        S.dma("sp", u_d[c0 + j * 128:c0 + (j + 1) * 128, tc * 512:(tc + 1) * 512], stb[k][:, :], r=["cstb%d" % k], w=["u_d"])
            run_blocks(blocks(w_up_d, D, 4 * D), body_up)
            S.flush()
        with contextlib.ExitStack() as C7:
            acc = sb(C7, "dacc", [128, TT, 2048])
            ub = [sb(C7, "dub%d" % i, [128, 16, T], BF16) for i in range(2)]
            dw = [sb(C7, "ddw%d" % i, [128, 16, 512], BF16) for i in range(2)]
            xr = [sb(C7, "dxr%d" % i, [128, 512]) for i in range(2)]
            uv = u_d.rearrange("(c p) t -> p c t", p=128)
            wv2 = w_dn_d.rearrange("(c p) n -> p c n", p=128)
            ui = 0
            wi2 = 0
            evi2 = 0
            for half in range(2):
                for ks in range(8):
                    u_ = ub[ui % 2]
                    un = "dub%d" % (ui % 2)
                    ui += 1
                    S.dma("sp", u_[:, :, :], uv[:, ks * 16:(ks + 1) * 16, :], w=[un])
                    for n4 in range(4):
                        w_ = dw[wi2 % 2]
                        wn = "ddw%d" % (wi2 % 2)
                        wi2 += 1
                        c0 = half * 2048 + n4 * 512
                        S.dma("pool", w_[:, :, :], wv2[:, ks * 16:(ks + 1) * 16, c0:c0 + 512], w=[wn])
                        for ti in range(TT):
                            b = nb()
                            for kc in range(16):
                                S.i("pe", "matmul", dict(r=[un, wn], w=["ps%d" % b]), ps[b][:, :], u_[:, kc, ti * 128:(ti + 1) * 128],
                                    w_[:, kc, :], start=(kc == 0), stop=(kc == 15))
                            an = "dacc%d_%d" % (ti, n4)
                            a_ = acc[:, ti, n4 * 512:(n4 + 1) * 512]
                            if ks == 0:
                                evi2 += 1
                                if evi2 % 2:
                                    S.i("act", "copy", dict(w=["ps%d" % b, an]), out=a_, in_=ps[b][:, :])
                                else:
                                    S.i("dve", "tensor_copy", dict(w=["ps%d" % b, an]), out=a_, in_=ps[b][:, :])
                            else:
                                S.i("dve", "tensor_tensor", dict(w=["ps%d" % b, an]), out=a_, in0=ps[b][:, :], in1=a_, op=ALU.add)
                for n4 in range(4):
                    c0 = half * 2048 + n4 * 512
                    for ti in range(TT):
                        k = (ti + n4) % 2
                        an = "dacc%d_%d" % (ti, n4)
                        a_ = acc[:, ti, n4 * 512:(n4 + 1) * 512]
                        S.dma("sp", xr[k][:, :], h_d[ti * 128:(ti + 1) * 128, c0:c0 + 512], r=["hd%d_%d" % (ti, c0)], w=["dxr%d" % k])
                        S.i("dve", "tensor_tensor", dict(r=[an], w=["dxr%d" % k]), out=xr[k][:, :], in0=a_, in1=xr[k][:, :], op=ALU.add)
                        S.dma("sp", h_d[ti * 128:(ti + 1) * 128, c0:c0 + 512], xr[k][:, :], r=["dxr%d" % k], w=["hd%d_%d" % (ti, c0)])
            S.flush()
        with contextlib.ExitStack() as C8:
            ft = [sb(C8, "ft%d" % i, [128, D]) for i in range(2)]
            fj = sb(C8, "fj", [128, D], BF16)
            fw = sb(C8, "fw", [128, D])
            fs = sb(C8, "fs", [128, TT])
            S.dma("sp", fw[:, :], nw_f.partition_broadcast(128), w=["fw"])
            for i in range(TT):
                b = i % 2
                S.dma("sp", ft[b][:, :], h_d[i * 128:(i + 1) * 128, :], w=["ft%d" % b])
                S.i("act", "activation", dict(r=["ft%d" % b], w=["fj", "fs%d" % i]), out=fj[:, :], in_=ft[b][:, :], func=AF.Square,
                    accum_out=fs[:, i:i + 1])
                rstd_from_ss("act", fs[:, i:i + 1], fs[:, i:i + 1], D, [], ["fs%d" % i])
                S.i("dve", "scalar_tensor_tensor", dict(r=["fs%d" % i, "fw"], w=["ft%d" % b]), out=ft[b][:, :], in0=ft[b][:, :],
                    scalar=fs[:, i:i + 1], in1=fw[:, :], op0=ALU.mult, op1=ALU.mult)
                S.dma("sp", out_d[i * 128:(i + 1) * 128, :], ft[b][:, :], r=["ft%d" % b], w=["out"])
            S.flush()
    return nc


def prep(inp):
    f = lambda a: np.ascontiguousarray(np.asarray(a, dtype=np.float32))
    x = f(inp["x"])[0]
    pos = np.asarray(inp["positions"]).astype(np.int32)[0]
    w_in = f(inp["w_in"])[0]
    c_q, c_kv, k_pe, qkv, a_l, b_l, gate = np.split(w_in, np.cumsum([1024, 512, 64, 6144, 16, 16])[:], axis=1)
    rot = np.concatenate([k_pe[:, 32:], k_pe[:, :32]], axis=1)
    wq = f(inp["mla_w_q_b"])[0].reshape(1024, 16, 192)
    wq_n = np.ascontiguousarray(wq[:, :, :128].reshape(1024, 2048))
    wq_pe = wq[:, :, 128:]
    wq_rot = np.concatenate([wq_pe[:, :, 32:], wq_pe[:, :, :32]], axis=2)
    wkv = f(inp["mla_w_kv_b"])[0].reshape(512, 16, 256)
    shared = {
        "cst": make_consts(),
        "nw_attn": f(inp["attn_norm_w"])[0],
        "w_cq": f(np.concatenate([c_q, c_kv], axis=1)),
        "w_kpe": f(np.concatenate([k_pe, rot], axis=1)),
        "w_qkv": f(qkv), "w_gate": f(gate), "w_ab": f(np.concatenate([a_l, b_l], axis=1)),
        "nw_q": f(inp["mla_q_norm_w"])[0], "nw_kv": f(inp["mla_kv_norm_w"])[0],
        "w_qn": wq_n, "w_qpe": f(wq_pe.reshape(1024, 1024)), "w_qrot": f(wq_rot.reshape(1024, 1024)),
        "w_kn": f(wkv[:, :, :128].reshape(512, 2048)), "w_v": f(wkv[:, :, 128:].reshape(512, 2048)),
        "a_log": f(inp["gdn_a_log"]).reshape(1, 16), "dt_bias": f(inp["gdn_dt_bias"]).reshape(1, 16),
        "mem": f(inp["mem"])[0], "gnw": f(inp["gdn_norm_w"]).reshape(1, 128), "w_out": f(inp["w_out"])[0],
        "nw_x": f(inp["xattn_norm_w"])[0], "nw_m": f(inp["mem_norm_w"])[0],
        "wxq": f(inp["xattn_wq"])[0], "wxk": f(inp["xattn_wk"])[0], "wxv": f(inp["xattn_wv"])[0], "wxo": f(inp["xattn_wo"])[0],
        "nw_mlp": f(inp["mlp_norm_w"])[0], "w_up": f(inp["mlp_w_up"])[0], "w_dn": f(inp["mlp_w_down"])[0],
        "nw_f": f(inp["final_norm_w"]).reshape(1, D),
    }
    maps = []
    for c in range(NC):
        m = dict(shared)
        m["x"] = np.ascontiguousarray(x[c * T:(c + 1) * T])
        m["pos"] = np.ascontiguousarray(pos[c * T:(c + 1) * T]).reshape(1, T)
        cwf = f(inp["gdn_conv_w"])[0]
        cwc = np.zeros((128, 24), np.float32)
        for s_ in range(3):
            for h in range(2):
                ch0 = s_ * 2048 + (2 * c + h) * 128
                cwc[:, (s_ * 2 + h) * 4:(s_ * 2 + h) * 4 + 4] = cwf[:, ch0:ch0 + 128].T
        m["convw"] = cwc
        maps.append(m)
    return maps


_NC_CACHE = {}


def kernel(**inp):
    maps = prep(inp)
    if "nc" not in _NC_CACHE:
        _NC_CACHE["nc"] = build()
    res = run_bass_kernel_spmd(_NC_CACHE["nc"], maps, core_ids=list(range(NC)))
    out = np.concatenate([np.asarray(res.results[c]["out"], dtype=np.float32) for c in range(NC)], axis=0)
    return out.reshape(1, SEQ, D)
```
